# Optimizing a Trainium2 kernel written in Bass

```python
import math
import jax, jax.numpy as jnp
from jax import lax
import numpy as np

D_MODEL = 4096
BATCH = 4
SEQ = 2048
DEPTH = 1

D_MIX = D_MODEL
POOL_WINDOWS = (2, 4, 8, 16)
N_POOL_GROUPS = len(POOL_WINDOWS)
D_POOL = D_MIX // 4
POOL_GROUP = D_POOL // N_POOL_GROUPS
D_ATTN = D_MIX - D_POOL
HEAD_DIM = 128
N_HEADS = D_ATTN // (2 * HEAD_DIM)
D_QK = N_HEADS * 2 * HEAD_DIM
D_V = N_HEADS * 2 * HEAD_DIM
D_IN = D_POOL + 2 * D_QK + D_V
Q_BLOCK = 128
D_FF = ((8 * D_MODEL // 3 + 255) // 256) * 256
N_SUB = 3
N_MOD = 3
NORM_EPS = 1e-6

kernel_name = "hybrid_pool_diffattn_macaron_layer"


def rms_norm(x, g, eps=NORM_EPS):
    xf = x.astype(jnp.float32)
    y = xf * lax.rsqrt(jnp.mean(xf * xf, axis=-1, keepdims=True) + eps)
    return (y * g.astype(jnp.float32)).astype(x.dtype)


def modulate(h, shift, scale):
    return h * (1.0 + scale[:, None, :]) + shift[:, None, :]


def swiglu(h, w1, w3, w2):
    return (jax.nn.silu(h @ w1) * (h @ w3)) @ w2


def pool_mixer(u, w_pool, pool_scale):
    b, s, _ = u.shape
    ug = u.reshape(b, s, N_POOL_GROUPS, POOL_GROUP).astype(jnp.float32)
    cs = jnp.pad(jnp.cumsum(ug, axis=1), ((0, 0), (1, 0), (0, 0), (0, 0)))
    win = jnp.array(POOL_WINDOWS, dtype=jnp.int32)
    t = jnp.arange(s, dtype=jnp.int32)[:, None]
    lo = jnp.maximum(t + 1 - win[None, :], 0)
    grp = jnp.arange(N_POOL_GROUPS, dtype=jnp.int32)[None, :]
    window_sum = cs[:, 1:] - cs[:, lo, grp]
    count = jnp.minimum(t + 1, win[None, :]).astype(jnp.float32)
    y = window_sum / count[None, :, :, None] - ug
    y = jnp.einsum('bsgc,gcd->bsgd', y.astype(u.dtype), w_pool)
    return y.reshape(b, s, D_POOL) * pool_scale


def diff_attention(q, k, v, lam, subln_g, lambda_init):
    b, s = q.shape[:2]
    n_blk = s // Q_BLOCK
    qb = (q * (HEAD_DIM ** -0.5)).reshape(b, n_blk, Q_BLOCK, N_HEADS, 2, HEAD_DIM)
    qb = qb.transpose(1, 0, 2, 3, 4, 5)
    key_pos = jnp.arange(s, dtype=jnp.int32)
    starts = jnp.arange(n_blk, dtype=jnp.int32) * Q_BLOCK

    def block(args):
        q_blk, start = args
        scores = jnp.einsum('bqhmd,bkhmd->bhmqk', q_blk, k).astype(jnp.float32)
        q_pos = start + jnp.arange(Q_BLOCK, dtype=jnp.int32)
        causal = key_pos[None, :] <= q_pos[:, None]
        scores = jnp.where(causal, scores, -jnp.inf)
        p = jax.nn.softmax(scores, axis=-1)
        a = p[:, :, 0] - lam * p[:, :, 1]
        return jnp.einsum('bhqk,bkhe->bqhe', a.astype(v.dtype), v)

    o = lax.map(block, (qb, starts))
    o = o.transpose(1, 0, 2, 3, 4).reshape(b, s, N_HEADS, 2 * HEAD_DIM)
    o = rms_norm(o, subln_g) * (1.0 - lambda_init)
    return o.reshape(b, s, N_HEADS * 2 * HEAD_DIM)


def setup_inputs(seed: int = 0) -> dict:
    key = jax.random.key(seed)
    ks = jax.random.split(key, 24)
    f32 = jnp.float32
    nrm = lambda k, shape, s: jax.random.normal(k, shape, f32) * s
    L = DEPTH
    return {
        "x": nrm(ks[0], (BATCH, SEQ, D_MODEL), 1.0),
        "c": nrm(ks[1], (BATCH, D_MODEL), 1.0),
        "w_ada": nrm(ks[2], (L, D_MODEL, N_SUB * N_MOD * D_MODEL), D_MODEL ** -0.5),
        "b_ada": nrm(ks[3], (L, N_SUB * N_MOD * D_MODEL), 0.01),
        "pre_norm_g": 1.0 + nrm(ks[4], (L, N_SUB, D_MODEL), 0.02),
        "post_norm_g": 1.0 + nrm(ks[5], (L, N_SUB, D_MODEL), 0.02),
        "ffn1_w1": nrm(ks[6], (L, D_MODEL, D_FF), D_MODEL ** -0.5),
        "ffn1_w3": nrm(ks[7], (L, D_MODEL, D_FF), D_MODEL ** -0.5),
        "ffn1_w2": nrm(ks[8], (L, D_FF, D_MODEL), D_FF ** -0.5),
        "w_in": nrm(ks[9], (L, D_MODEL, D_IN), D_MODEL ** -0.5),
        "w_pool": nrm(ks[10], (L, N_POOL_GROUPS, POOL_GROUP, POOL_GROUP), POOL_GROUP ** -0.5),
        "pool_scale": 1.0 + nrm(ks[11], (L, D_POOL), 0.1),
        "lambda_q1": nrm(ks[12], (L, HEAD_DIM), 0.1),
        "lambda_k1": nrm(ks[13], (L, HEAD_DIM), 0.1),
        "lambda_q2": nrm(ks[14], (L, HEAD_DIM), 0.1),
        "lambda_k2": nrm(ks[15], (L, HEAD_DIM), 0.1),
        "subln_g": 1.0 + nrm(ks[16], (L, 2 * HEAD_DIM), 0.02),
        "w_out": nrm(ks[17], (L, D_MIX, D_MODEL), D_MIX ** -0.5),
        "ffn2_w1": nrm(ks[18], (L, D_MODEL, D_FF), D_MODEL ** -0.5),
        "ffn2_w3": nrm(ks[19], (L, D_MODEL, D_FF), D_MODEL ** -0.5),
        "ffn2_w2": nrm(ks[20], (L, D_FF, D_MODEL), D_FF ** -0.5),
    }


def reference(x, c, w_ada, b_ada, pre_norm_g, post_norm_g, ffn1_w1, ffn1_w3, ffn1_w2,
              w_in, w_pool, pool_scale, lambda_q1, lambda_k1, lambda_q2, lambda_k2,
              subln_g, w_out, ffn2_w1, ffn2_w3, ffn2_w2):
    b, s, _ = x.shape
    for l in range(DEPTH):
        lambda_init = 0.8 - 0.6 * math.exp(-0.3 * l)
        mod = (jax.nn.silu(c) @ w_ada[l] + b_ada[l]).reshape(b, N_SUB, N_MOD, D_MODEL)
        shift, scale, gate = mod[:, :, 0], mod[:, :, 1], mod[:, :, 2]

        h = modulate(rms_norm(x, pre_norm_g[l, 0]), shift[:, 0], scale[:, 0])
        f = rms_norm(swiglu(h, ffn1_w1[l], ffn1_w3[l], ffn1_w2[l]), post_norm_g[l, 0])
        x = x + 0.5 * gate[:, 0, None, :] * f

        h = modulate(rms_norm(x, pre_norm_g[l, 1]), shift[:, 1], scale[:, 1])
        proj = h @ w_in[l]
        u = proj[..., :D_POOL]
        q = proj[..., D_POOL:D_POOL + D_QK].reshape(b, s, N_HEADS, 2, HEAD_DIM)
        k = proj[..., D_POOL + D_QK:D_POOL + 2 * D_QK].reshape(b, s, N_HEADS, 2, HEAD_DIM)
        v = proj[..., D_POOL + 2 * D_QK:].reshape(b, s, N_HEADS, 2 * HEAD_DIM)
        lam = (jnp.exp(jnp.sum(lambda_q1[l].astype(jnp.float32) * lambda_k1[l].astype(jnp.float32)))
               - jnp.exp(jnp.sum(lambda_q2[l].astype(jnp.float32) * lambda_k2[l].astype(jnp.float32)))
               + lambda_init)
        y_pool = pool_mixer(u, w_pool[l], pool_scale[l])
        y_attn = diff_attention(q, k, v, lam, subln_g[l], lambda_init)
        mixed = jnp.concatenate([y_pool, y_attn.astype(y_pool.dtype)], axis=-1) @ w_out[l]
        x = x + gate[:, 1, None, :] * rms_norm(mixed, post_norm_g[l, 1])

        h = modulate(rms_norm(x, pre_norm_g[l, 2]), shift[:, 2], scale[:, 2])
        f = rms_norm(swiglu(h, ffn2_w1[l], ffn2_w3[l], ffn2_w2[l]), post_norm_g[l, 2])
        x = x + 0.5 * gate[:, 2, None, :] * f
    return x
```

```python
import contextlib
import numpy as np
import ml_dtypes
import concourse.bass as bass
import concourse.mybir as mybir
from concourse.bass_utils import run_bass_kernel_spmd

F32 = mybir.dt.float32
BF16 = mybir.dt.bfloat16
ALU = mybir.AluOpType
AF = mybir.ActivationFunctionType
AX = mybir.AxisListType

D = 4096
DFF = 11008
T = 1024
NT = 8
KC = 32
EPS = 1e-6
NCORES = 8
ADA_SH = 4608
CSPLIT = 1024
STAGE = 3


class Prog:
    def __init__(self, nc, stack):
        self.nc = nc
        self.stack = stack
        self.ops = []
        self.last_w = {}
        self.readers = {}
        self.dma_sems = {}
        self.dma_cnt = {}
        self.eng_sem = {}
        for e in ("pe", "act", "dve", "pool"):
            self.eng_sem[e] = stack.enter_context(nc.semaphore("s_" + e))

    def _deps(self, reads, writes):
        deps = set()
        for k in reads:
            if k in self.last_w:
                deps.add(self.last_w[k])
        for k in writes:
            if k in self.last_w:
                deps.add(self.last_w[k])
            for r in self.readers.get(k, {}).values():
                deps.add(r)
        return deps

    def _commit(self, i, reads, writes, eng=None):
        for k in writes:
            self.last_w[k] = i
            self.readers[k] = {}
        for k in reads:
            if k not in writes:
                self.readers.setdefault(k, {})[eng if eng is not None else ("d", i)] = i

    def op(self, eng, fn, reads=(), writes=()):
        i = len(self.ops)
        deps = self._deps(reads, writes)
        self.ops.append(dict(eng=eng, fn=fn, deps=deps, kind="c"))
        self._commit(i, reads, writes, eng)
        return i

    def dma(self, q, out, in_, sem, reads=(), writes=()):
        if sem not in self.dma_sems:
            self.dma_sems[sem] = self.stack.enter_context(self.nc.semaphore("d%d" % len(self.dma_sems)))
            self.dma_cnt[sem] = 0
        self.dma_cnt[sem] += 16
        i = len(self.ops)
        deps = self._deps(reads, writes)
        self.ops.append(dict(eng=q, fn=lambda e: e.dma_start(out=out, in_=in_), deps=deps, kind="d",
                             sem=self.dma_sems[sem], cnt=self.dma_cnt[sem]))
        self._commit(i, reads, writes)
        return i

    def cc(self, fn, sem, reads=(), writes=()):
        if sem not in self.dma_sems:
            self.dma_sems[sem] = self.stack.enter_context(self.nc.semaphore("d%d" % len(self.dma_sems)))
            self.dma_cnt[sem] = 0
        self.dma_cnt[sem] += 1
        i = len(self.ops)
        deps = self._deps(reads, writes)
        self.ops.append(dict(eng="pool", fn=fn, deps=deps, kind="cc", sem=self.dma_sems[sem],
                             cnt=self.dma_cnt[sem]))
        self._commit(i, reads, writes)
        return i

    def barrier(self):
        i = len(self.ops)
        last = {}
        for j, o in enumerate(self.ops):
            if o["kind"] == "c":
                last[("e", o["eng"])] = j
            elif o["kind"] != "b":
                last[("s", id(o["sem"]))] = j
        deps = set(last.values())
        for e in ("pe", "act", "dve", "pool", "sp"):
            self.ops.append(dict(eng=e, fn=None, deps=set(deps), kind="b"))
        self.last_w = {}
        self.readers = {}

    def emit(self):
        nc = self.nc
        ops = self.ops
        need = [False] * len(ops)
        for o in ops:
            for d in o["deps"]:
                p = ops[d]
                if p["kind"] == "c":
                    if p["eng"] == o["eng"] and o["kind"] in ("c", "b") and p["eng"] == "pe":
                        continue
                    need[d] = True
        cnt = {e: 0 for e in self.eng_sem}
        for i, o in enumerate(ops):
            if o["kind"] == "c" and need[i]:
                cnt[o["eng"]] += 1
                o["sem"] = self.eng_sem[o["eng"]]
                o["cnt"] = cnt[o["eng"]]
        engs = {"pe": "tensor", "act": "scalar", "dve": "vector", "pool": "gpsimd", "sp": "sync"}

        def run(ename, e):
            waited = {}
            for i, o in enumerate(ops):
                if o["eng"] != ename:
                    continue
                mx = {}
                for d in o["deps"]:
                    p = ops[d]
                    if p["kind"] == "b":
                        continue
                    if p["kind"] == "c":
                        if not need[d]:
                            continue
                        if p["eng"] == ename and ename == "pe":
                            continue
                    s, c = p["sem"], p["cnt"]
                    if id(s) not in mx or mx[id(s)][1] < c:
                        mx[id(s)] = (s, c)
                for s, c in mx.values():
                    if waited.get(id(s), 0) >= c:
                        continue
                    waited[id(s)] = c
                    e.wait_ge(s, c)
                if o["kind"] == "b":
                    continue
                ins = o["fn"](e)
                if o["kind"] == "d":
                    ins.then_inc(o["sem"], 16)
                elif o["kind"] == "cc":
                    ins.then_inc(o["sem"])
                elif need[i]:
                    ins.then_inc(o["sem"], 1)
            if ename == "sp":
                for k, s in self.dma_sems.items():
                    e.wait_ge(s, self.dma_cnt[k])

        with nc.Block() as block:
            @block.tensor
            def _(e):
                run("pe", e)

            @block.scalar
            def _(e):
                run("act", e)

            @block.vector
            def _(e):
                run("dve", e)

            @block.gpsimd
            def _(e):
                run("pool", e)

            @block.sync
            def _(e):
                run("sp", e)


class _Stop(Exception):
    pass


def build_nc(stage=STAGE, stop=None):
    nc = bass.Bass("TRN2", target_bir_lowering=False)

    def din(name, shape, dt=F32):
        return nc.dram_tensor(name, list(shape), dt, kind="ExternalInput").ap()

    x_in = din("x", [T, D])
    cT_in = din("cT", [128, KC, 4])
    wada = din("wada", [D, ADA_SH])
    bada = din("bada", [ADA_SH])
    sel_in = din("sel", [32, 8, 128])
    pre_g = din("pre_g", [3, D])
    post_g = din("post_g", [3, D])
    w1 = [din("w1a", [D, DFF]), din("w1b", [D, DFF])]
    w3 = [din("w3a", [D, DFF]), din("w3b", [D, DFF])]
    w2 = [din("w2a", [DFF, D]), din("w2b", [DFF, D])]
    w_in = din("w_in", [D, 10240])
    w_pool = din("w_pool", [4, 256, 256])
    pool_scale = din("pool_scale", [128, 8])
    lam_in = din("lam", [4, 128])
    subln_g = din("subln_g", [256])
    w_out = din("w_out", [D, D])
    ident_in = din("ident", [128, 128], BF16)
    tril_in = din("tril", [128, 128], BF16)
    pc_in = din("percore", [128, 4])
    invc_in = din("invcnt", [4 * T])
    out = nc.dram_tensor("out", [T, D], F32, kind="ExternalOutput").ap()

    mod_src = nc.dram_tensor("mod_src", [4, ADA_SH], F32).ap()
    mod_dst = nc.dram_tensor("mod_dst", [32, ADA_SH], F32).ap()
    KVR = 7168
    kv_src = nc.dram_tensor("kv_src", [KVR, T], BF16).ap()
    kv_dst = nc.dram_tensor("kv_dst", [2 * KVR, T], BF16).ap()
    gT = nc.dram_tensor("gT", [DFF, T], BF16).ap()
    x1 = nc.dram_tensor("x1s", [T, D], F32).ap()
    x2 = nc.dram_tensor("x2s", [T, D], F32).ap()

    with contextlib.ExitStack() as stack:
        P = Prog(nc, stack)

        _uid = [0]

        def uq(name):
            _uid[0] += 1
            return "%s_u%d" % (name, _uid[0])

        def sb(name, shape, dt):
            return nc.sbuf_tensor(uq(name), list(shape), dt)

        class _Scope:
            def __init__(self, st):
                self.st = st

            def sb(self, name, shape, dt):
                return self.st.enter_context(nc.sbuf_tensor(uq(name), list(shape), dt))

            def ps(self, name, shape, dt=F32):
                return self.st.enter_context(nc.psum_tensor(uq(name), list(shape), dt))

        @contextlib.contextmanager
        def scope():
            with contextlib.ExitStack() as st:
                yield _Scope(st)

        ident = stack.enter_context(sb("ident", [128, 128], BF16))
        tril = stack.enter_context(sb("tril", [128, 128], BF16))
        pc = stack.enter_context(sb("pc", [128, 4], F32))
        sel = stack.enter_context(sb("sel", [32, 8, 128], F32))
        zero1 = stack.enter_context(sb("zero1", [128, 1], F32))
        stat = stack.enter_context(sb("stat", [128, 64], F32))
        P.dma("sp", ident[:], ident_in, "c0", writes=["ident"])
        P.dma("sp", tril[:], tril_in, "c1", writes=["tril"])
        P.dma("sp", pc[:], pc_in, "c2", writes=["pc"])
        P.dma("sp", sel[:], sel_in, "c3", writes=["sel"])
        P.op("dve", lambda e: e.memset(zero1[:], 0.0), writes=["zero1"])
        epsb = stack.enter_context(sb("epsb", [128, 1], F32))
        P.op("dve", lambda e: e.memset(epsb[:], EPS), writes=["epsb"])

        with scope() as S_:
            cTt = S_.sb("cTt", [128, KC, 4], F32)
            sc = S_.sb("sc", [128, KC, 4], BF16)
            wra = S_.sb("wra", [128, 2, KC, 512], BF16)
            biasa = S_.sb("biasa", [4, ADA_SH], F32)
            moda = S_.sb("moda", [4, ADA_SH], F32)
            pa = S_.ps("pa", [128, 2, 512])
            P.dma("sp", cTt[:], cT_in, "a0", writes=["cTt"])
            P.dma("sp", biasa[:], bada.partition_broadcast(4), "a1", writes=["biasa"])
            P.op("act", lambda e: e.activation(out=sc[:], in_=cTt[:], func=AF.Silu), reads=["cTt"], writes=["sc"])
            for ct in range(9):
                sl = ct % 2
                P.dma("pool", wra[:, sl], wada[:, ct * 512:(ct + 1) * 512].rearrange("(kc p) f -> p kc f", p=128),
                      ("wra", sl), writes=[("wra", sl)])
                for kc in range(KC):
                    P.op("pe", lambda e, sl=sl, kc=kc: e.matmul(pa[0:4, sl, :], lhsT=sc[:, kc, :], rhs=wra[:, sl, kc, :],
                                                                start=(kc == 0), stop=(kc == KC - 1)),
                         reads=["sc", ("wra", sl)], writes=[("pa", sl)])
                P.op("dve", lambda e, sl=sl, ct=ct: e.tensor_tensor(out=moda[:, ct * 512:(ct + 1) * 512], in0=pa[0:4, sl, :],
                                                                     in1=biasa[:, ct * 512:(ct + 1) * 512], op=ALU.add),
                     reads=["biasa"], writes=[("pa", sl), ("moda", ct)])
            P.dma("sp", mod_src, moda[:], "a2", reads=[("moda", ct) for ct in range(9)], writes=["mod_src"])
            P.cc(lambda e: e.collective_compute("AllGather", ALU.bypass, replica_groups=[list(range(NCORES))],
                                                ins=[mod_src.opt()], outs=[mod_dst.opt()]),
                 "cc0", reads=["mod_src"], writes=["mod_dst"])
        P.barrier()

        def dbg_stop(tag, src_ap=None, rows=None, cols=None):
            if stop != tag:
                return
            if src_ap is not None:
                with scope() as S_:
                    dt_ = S_.sb("dbgt", [rows, cols], F32)
                    P.dma("pool", dt_[:], src_ap, "dbg0", writes=["dbgt"])
                    P.dma("sp", out[0:rows, 0:cols], dt_[:], "dbg1", reads=["dbgt"], writes=["dbgo"])
                    P.barrier()
            raise _Stop()

        def main_body():
            def rstd_ops(ssq_ap, out_ap, n, keys_r, keys_w):
                P.op("act", lambda e: e.activation(out=out_ap, in_=ssq_ap, func=AF.Ln, bias=epsb[:, 0:1], scale=1.0 / n),
                     reads=list(keys_r) + ["epsb"], writes=keys_w)
                P.op("act", lambda e: e.activation(out=out_ap, in_=out_ap, func=AF.Exp, scale=-0.5),
                     reads=keys_w, writes=keys_w)

            def bc_vectors(jobs, pbank, tagbase):
                with scope() as S_:
                    Gm = S_.sb("Gm", [32, ADA_SH], F32)
                    P.dma("sp", Gm[:], mod_dst, "g0", reads=["mod_dst"], writes=["Gm"])
                    it = 0
                    for (v, evac) in jobs:
                        for dt_ in range(8):
                            jt = v * 8 + dt_
                            r, lt = jt // 9, jt % 9
                            b = it % 2
                            it += 1
                            P.op("pe", lambda e, b=b, r=r, lt=lt: e.matmul(pbank[:, b, :], lhsT=sel[:, r, :],
                                                                           rhs=Gm[:, lt * 512:(lt + 1) * 512], start=True, stop=True),
                                 reads=["Gm", "sel"], writes=[(tagbase, b)])
                            eng, fn, rd, wr = evac(dt_, pbank[:, b, :])
                            P.op(eng, fn, reads=rd, writes=list(wr) + [(tagbase, b)])
                    P.barrier()

            def prologue(s, xsrc, hT):
                with scope() as S_:
                    Avec = S_.sb("Avec", [128, D], F32)
                    Svec = S_.sb("Svec", [128, D], F32)
                    with scope() as S_:
                        pgb = S_.sb("pgb", [128, D], F32)
                        pbv = S_.ps("pbv", [128, 2, 512])
                        P.dma("sp", pgb[:], pre_g[s].partition_broadcast(128), "g1", writes=["pgb"])

                        def evA(dt_, p):
                            sl_ = slice(dt_ * 512, (dt_ + 1) * 512)
                            return ("dve", lambda e: e.scalar_tensor_tensor(out=Avec[:, sl_], in0=p, scalar=1.0, in1=pgb[:, sl_],
                                                                            op0=ALU.add, op1=ALU.mult), ["pgb"], [("Avec", dt_)])

                        def evS(dt_, p):
                            sl_ = slice(dt_ * 512, (dt_ + 1) * 512)
                            return ("act", lambda e: e.copy(out=Svec[:, sl_], in_=p), [], [("Svec", dt_)])
                        bc_vectors([(3 * s + 1, evA), (3 * s + 0, evS)], pbv, "pbv")
                    with scope() as S_:
                        xt = S_.sb("xt", [128, 2, D], F32)
                        htok = S_.sb("htok", [128, 2, D], BF16)
                        ptr = S_.ps("ptr", [128, 2, 1024], BF16)
                        P.op("dve", lambda e: e.memset(stat[:, 0:16], 0.0), writes=[("ssq", t_) for t_ in range(8)] + [("rstd", t_) for t_ in range(8)])
                        for tt in range(NT):
                            b = tt % 2
                            xtk = [("xt", b, 0), ("xt", b, 1)]
                            htk = [("htok", b, 0), ("htok", b, 1)]
                            P.dma("sp", xt[:, b], xsrc[tt * 128:(tt + 1) * 128, :], ("xt", b), reads=[xsrc.name],
                                  writes=xtk)
                            P.op("act", lambda e, b=b, tt=tt: e.activation(out=htok[:, b], in_=xt[:, b], func=AF.Square,
                                                                            accum_out=stat[:, tt:tt + 1]),
                                 reads=xtk, writes=htk + [("ssq", tt)])
                            rstd_ops(stat[:, tt:tt + 1], stat[:, 8 + tt:9 + tt], D, [("ssq", tt)], [("rstd", tt)])
                            P.op("dve", lambda e, b=b, tt=tt: e.scalar_tensor_tensor(
                                out=xt[:, b], in0=xt[:, b], scalar=stat[:, 8 + tt:9 + tt], in1=Avec[:], op0=ALU.mult, op1=ALU.mult),
                                 reads=[("rstd", tt)], writes=xtk)
                            for ei_, (eng_, c0_, c1_) in enumerate((("dve", 0, CSPLIT), ("pool", CSPLIT, D))):
                                P.op(eng_, lambda e, b=b, c0_=c0_, c1_=c1_: e.tensor_tensor(
                                    out=htok[:, b, c0_:c1_], in0=xt[:, b, c0_:c1_], in1=Svec[:, c0_:c1_], op=ALU.add),
                                     reads=[("xt", b, ei_)], writes=[("htok", b, ei_)])
                            for g in range(4):
                                pb = g % 2
                                for j in range(8):
                                    kc = g * 8 + j
                                    P.op("pe", lambda e, b=b, kc=kc, pb=pb, j=j: e.transpose(
                                        ptr[:, pb, j * 128:(j + 1) * 128], htok[:, b, kc * 128:(kc + 1) * 128], ident[:]),
                                         reads=htk + ["ident"], writes=[("ptr", pb)])
                                eng = "act" if g % 2 == 0 else "dve"
                                src = ptr[:, pb, :].rearrange("p (j t) -> p j t", t=128)
                                dst = hT[:, g * 8:(g + 1) * 8, tt * 128:(tt + 1) * 128]
                                if eng == "act":
                                    P.op("act", lambda e, src=src, dst=dst: e.copy(out=dst, in_=src),
                                         writes=[("ptr", pb), ("hT", g, tt)])
                                else:
                                    P.op("dve", lambda e, src=src, dst=dst: e.tensor_copy(out=dst, in_=src),
                                         writes=[("ptr", pb), ("hT", g, tt)])
                        P.barrier()

            def hT_keys(half=None, tt=None):
                if tt is not None:
                    return [("hT", g, tt) for g in range(4)]
                return [("hT", g, t_) for g in range(4) for t_ in range(half * 4, half * 4 + 4)]

            def epilogue(s, F, tts, xsrc, xdst, gscale):
                with scope() as S_:
                    Gvec = S_.sb("Gvec", [128, D], F32)
                    with scope() as S_:
                        pgb2 = S_.sb("pgb2", [128, D], F32)
                        pbv2 = S_.ps("pbv2", [128, 2, 512])
                        P.dma("sp", pgb2[:], post_g[s].partition_broadcast(128), "g2", writes=["pgb2"])

                        def evG(dt_, p):
                            sl_ = slice(dt_ * 512, (dt_ + 1) * 512)
                            return ("dve", lambda e: e.scalar_tensor_tensor(out=Gvec[:, sl_], in0=p, scalar=gscale, in1=pgb2[:, sl_],
                                                                            op0=ALU.mult, op1=ALU.mult), ["pgb2"], [("Gvec", dt_)])
                        bc_vectors([(3 * s + 2, evG)], pbv2, "pbv2")
                    with scope() as S_:
                        xe = S_.sb("xe", [128, 2, D], F32)
                        junk2 = S_.sb("junk2", [128, D], BF16)
                        P.op("dve", lambda e: e.memset(stat[:, 16:32], 0.0), writes=[("ssq2", t_) for t_ in range(8)] + [("rstd2", t_) for t_ in range(8)])
                        for i, tt in enumerate(tts):
                            b = i % 2
                            xek = [("xe", b, 0), ("xe", b, 1)]
                            Fk = [("F", i, 0), ("F", i, 1)]
                            P.dma("sp", xe[:, b], xsrc[tt * 128:(tt + 1) * 128, :], ("xe", b), reads=[xsrc.name],
                                  writes=xek)
                            P.op("act", lambda e, i=i: e.activation(out=junk2[:], in_=F[:, i, :], func=AF.Square,
                                                                     accum_out=stat[:, 16 + i:17 + i]),
                                 reads=[("F", i)] + Fk, writes=["junk2", ("ssq2", i)])
                            rstd_ops(stat[:, 16 + i:17 + i], stat[:, 24 + i:25 + i], D, [("ssq2", i)], [("rstd2", i)])
                            P.op("dve", lambda e, i=i: e.scalar_tensor_tensor(
                                out=F[:, i, :], in0=F[:, i, :], scalar=stat[:, 24 + i:25 + i], in1=Gvec[:], op0=ALU.mult, op1=ALU.mult),
                                 reads=[("rstd2", i)], writes=Fk)
                            for ei_, (eng_, c0_, c1_) in enumerate((("dve", 0, CSPLIT), ("pool", CSPLIT, D))):
                                P.op(eng_, lambda e, i=i, b=b, c0_=c0_, c1_=c1_: e.tensor_tensor(
                                    out=xe[:, b, c0_:c1_], in0=F[:, i, c0_:c1_], in1=xe[:, b, c0_:c1_], op=ALU.add),
                                     reads=[("F", i, ei_)], writes=[("xe", b, ei_)])
                            P.dma("sp", xdst[tt * 128:(tt + 1) * 128, :], xe[:, b], ("xeo", b), reads=xek,
                                  writes=[("xo", xdst.name, tt)])
                        P.barrier()

            def ffn(s, idx, xsrc, xdst):
                W1, W3, W2 = w1[idx], w3[idx], w2[idx]
                with scope() as S_:
                    hT = S_.sb("hT", [128, KC, T], BF16)
                    prologue(s, xsrc, hT)
                    with scope() as S_:
                        wru = S_.sb("wru", [128, 2, 2, KC, 256], BF16)
                        sg = S_.sb("sg", [128, 2, 512], F32)
                        gt = S_.sb("gt", [128, 3, T], BF16)
                        pu = S_.ps("pu", [128, 4, 512])
                        it = 0
                        for st in range(DFF // 256):
                            sl = st % 2
                            P.dma("pool", wru[:, sl, 0], W1[:, st * 256:(st + 1) * 256].rearrange("(kc p) f -> p kc f", p=128),
                                  ("wru", sl), writes=[("wru", sl)])
                            P.dma("pool", wru[:, sl, 1], W3[:, st * 256:(st + 1) * 256].rearrange("(kc p) f -> p kc f", p=128),
                                  ("wru", sl), writes=[("wru", sl)])
                            for j in range(2):
                                ffc = st * 2 + j
                                gb = ffc % 3
                                for half in range(2):
                                    pb = (it % 2) * 2
                                    sgb = it % 2
                                    it += 1
                                    for m in range(2):
                                        for kc in range(KC):
                                            P.op("pe", lambda e, sl=sl, m=m, kc=kc, j=j, half=half, pb=pb: e.matmul(
                                                pu[:, pb + m, :], lhsT=wru[:, sl, m, kc, j * 128:(j + 1) * 128],
                                                rhs=hT[:, kc, half * 512:(half + 1) * 512], start=(kc == 0), stop=(kc == KC - 1)),
                                                 reads=[("wru", sl)] + (hT_keys(half) if kc == 0 else []), writes=[("pu", pb + m)])
                                    P.op("act", lambda e, pb=pb, sgb=sgb: e.activation(out=sg[:, sgb, :], in_=pu[:, pb, :], func=AF.Silu),
                                         writes=[("pu", pb), ("sg", sgb)])
                                    P.op("dve", lambda e, pb=pb, sgb=sgb, gb=gb, half=half: e.tensor_tensor(
                                        out=gt[:, gb, half * 512:(half + 1) * 512], in0=sg[:, sgb, :], in1=pu[:, pb + 1, :], op=ALU.mult),
                                         reads=[("sg", sgb)], writes=[("pu", pb + 1), ("gt", gb)])
                                P.dma("sp", gT[ffc * 128:(ffc + 1) * 128, :], gt[:, gb, :], ("gto", gb), reads=[("gt", gb)],
                                      writes=[("gT", ffc)])
                        P.barrier()
                with scope() as S_:
                    F = S_.sb("F", [128, NT, D], F32)
                    with scope() as S_:
                        gbuf = S_.sb("gbuf", [128, 2, 8, T], BF16)
                        wbuf = S_.sb("wbuf", [128, 2, 8, 512], BF16)
                        pd = S_.ps("pd", [128, 8, 512])
                        groups = [(g0, min(8, 86 - g0)) for g0 in range(0, 86, 8)]
                        it = 0
                        for ct in range(8):
                            for gi, (g0, gn) in enumerate(groups):
                                sl = it % 2
                                it += 1
                                P.dma("sp", gbuf[:, sl, 0:gn, :], gT[g0 * 128:(g0 + gn) * 128, :].rearrange("(j p) t -> p j t", p=128),
                                      ("gbuf", sl), writes=[("gbuf", sl)])
                                P.dma("pool", wbuf[:, sl, 0:gn, :],
                                      W2[g0 * 128:(g0 + gn) * 128, ct * 512:(ct + 1) * 512].rearrange("(j p) f -> p j f", p=128),
                                      ("wbuf", sl), writes=[("wbuf", sl)])
                                for j in range(gn):
                                    for tt in range(NT):
                                        first = (gi == 0 and j == 0)
                                        last = (gi == len(groups) - 1 and j == gn - 1)
                                        P.op("pe", lambda e, sl=sl, j=j, tt=tt, first=first, last=last: e.matmul(
                                            pd[:, tt, :], lhsT=gbuf[:, sl, j, tt * 128:(tt + 1) * 128], rhs=wbuf[:, sl, j, :],
                                            start=first, stop=last),
                                             reads=[("gbuf", sl), ("wbuf", sl)], writes=[("pd", tt)])
                            for tt in range(NT):
                                dst = F[:, tt, ct * 512:(ct + 1) * 512]
                                if tt % 2 == 0:
                                    P.op("act", lambda e, tt=tt, dst=dst: e.copy(out=dst, in_=pd[:, tt, :]),
                                         writes=[("pd", tt), ("F", tt)])
                                else:
                                    P.op("dve", lambda e, tt=tt, dst=dst: e.tensor_copy(out=dst, in_=pd[:, tt, :]),
                                         writes=[("pd", tt), ("F", tt)])
                        P.barrier()
                    epilogue(s, F, list(range(NT)), xsrc, xdst, 0.5)

            ffn(0, 0, x_in, out if stage == 1 else x1)

            if stage >= 2:
                mixer_dst = out if stage == 2 else x2
                def vown(hd_):
                    return kv_src[3072 + hd_ * 256:3072 + (hd_ + 1) * 256, :].rearrange("r (a c) -> (r a) c", c=256)

                def vrem(hd_):
                    r0 = (12 + hd_) * 512
                    return kv_dst[r0:r0 + 256, :].rearrange("r (a c) -> (r a) c", c=256)
                with scope() as S_:
                    hT = S_.sb("hT", [128, KC, T], BF16)
                    mixT = hT
                    with scope() as S_:
                        qT = S_.sb("qT", [128, 24, T], BF16)
                        prologue(1, x1, hT)
                        with scope() as S_:
                            wri = S_.sb("wri", [128, 2, KC, 512], BF16)
                            kst = S_.sb("kst", [128, 3, T], BF16)
                            vst = S_.sb("vst", [128, 3, 512], BF16)
                            pm = S_.ps("pm", [128, 4, 512])
                            order = [("u", c) for c in range(0, 2)] + [("k", c) for c in range(8, 14)] + \
                                    [("v", c) for c in range(14, 20)] + [("q", c) for c in range(2, 8)]
                            itp = 0
                            nk = 0
                            nv = 0
                            kvkeys = []

                            def kvkey():
                                kvkeys.append(("kvs", len(kvkeys)))
                                return kvkeys[-1]
                            for si, (kind, c) in enumerate(order):
                                sl = si % 2
                                P.dma("pool", wri[:, sl], w_in[:, c * 512:(c + 1) * 512].rearrange("(kc p) f -> p kc f", p=128),
                                      ("wri", sl), writes=[("wri", sl)])
                                if kind == "v":
                                    vs = c - 14
                                    for tt in range(NT):
                                        pb = itp % 4
                                        itp += 1
                                        for kc in range(KC):
                                            P.op("pe", lambda e, sl=sl, kc=kc, tt=tt, pb=pb: e.matmul(
                                                pm[:, pb, :], lhsT=hT[:, kc, tt * 128:(tt + 1) * 128], rhs=wri[:, sl, kc, :],
                                                start=(kc == 0), stop=(kc == KC - 1)),
                                                 reads=[("wri", sl)] + (hT_keys(tt=tt) if kc == 0 else []), writes=[("pm", pb)])
                                        vb = nv % 3
                                        nv += 1
                                        P.op("act", lambda e, pb=pb, vb=vb: e.copy(out=vst[:, vb, :], in_=pm[:, pb, :]),
                                             writes=[("pm", pb), ("vst", vb)])
                                        for hh in range(2):
                                            P.dma("sp", vown(vs * 2 + hh)[tt * 128:(tt + 1) * 128, :], vst[:, vb, hh * 256:(hh + 1) * 256],
                                                  ("vsto", vb), reads=[("vst", vb)], writes=[kvkey()])
                                else:
                                    for j in range(4):
                                        if kind in ("u", "k"):
                                            kb_ = nk % 3
                                            nk += 1
                                        for half in range(2):
                                            pb = itp % 4
                                            itp += 1
                                            for kc in range(KC):
                                                P.op("pe", lambda e, sl=sl, kc=kc, j=j, half=half, pb=pb: e.matmul(
                                                    pm[:, pb, :], lhsT=wri[:, sl, kc, j * 128:(j + 1) * 128],
                                                    rhs=hT[:, kc, half * 512:(half + 1) * 512], start=(kc == 0), stop=(kc == KC - 1)),
                                                     reads=[("wri", sl)] + (hT_keys(half) if kc == 0 else []), writes=[("pm", pb)])
                                            if kind == "q":
                                                fc = (c - 2) * 4 + j
                                                dst = qT[:, fc, half * 512:(half + 1) * 512]
                                                P.op("act", lambda e, pb=pb, dst=dst: e.activation(out=dst, in_=pm[:, pb, :], func=AF.Copy,
                                                                                                   scale=float(128 ** -0.5)),
                                                     writes=[("pm", pb), ("qT", fc)])
                                            else:
                                                dst = kst[:, kb_, half * 512:(half + 1) * 512]
                                                if half == 0:
                                                    P.op("act", lambda e, pb=pb, dst=dst: e.copy(out=dst, in_=pm[:, pb, :]),
                                                         writes=[("pm", pb), ("kst", kb_)])
                                                else:
                                                    P.op("dve", lambda e, pb=pb, dst=dst: e.tensor_copy(out=dst, in_=pm[:, pb, :]),
                                                         writes=[("pm", pb), ("kst", kb_)])
                                        if kind == "k":
                                            row = ((c - 8) * 4 + j) * 128
                                            P.dma("sp", kv_src[row:row + 128, :], kst[:, kb_, :], ("ksto", kb_), reads=[("kst", kb_)],
                                                  writes=[kvkey()])
                                        elif kind == "u":
                                            row = 6144 + (c * 4 + j) * 128
                                            P.dma("sp", kv_src[row:row + 128, :], kst[:, kb_, :], ("ksto", kb_), reads=[("kst", kb_)],
                                                  writes=[kvkey()])
                                if kind == "v" and c == 19:
                                    for ck in range(28):
                                        P.cc(lambda e, ck=ck: e.collective_compute(
                                            "AllGather", ALU.bypass, replica_groups=[[0, 1], [2, 3], [4, 5], [6, 7]],
                                            ins=[kv_src[ck * 256:(ck + 1) * 256, :].opt()],
                                            outs=[kv_dst[ck * 512:(ck + 1) * 512, :].opt()]),
                                             "cc1", reads=list(kvkeys) if ck == 0 else [], writes=[("kv_dst", ck)])
                            P.barrier()
                        if True:
                            with scope() as S_:
                                kth = S_.sb("kth", [128, 2, 2, 2, T], BF16)
                                vth = S_.sb("vth", [128, 2, 16, 258], BF16)
                                E = S_.sb("E", [128, 3, 2, 256], BF16)
                                lamt = S_.sb("lamt", [128, 512], F32)
                                lamp = S_.sb("lamp", [128, 256], F32)
                                sg8 = S_.sb("sg8", [128, 256], F32)
                                o1 = S_.sb("o1", [128, 2, 256], F32)
                                yb = S_.sb("yb", [128, 2, 256], BF16)
                                junk3 = S_.sb("junk3", [128, 256], BF16)
                                st2 = S_.sb("st2", [128, 64], F32)
                                ub = S_.sb("ub", [128, 2, 1040], BF16)
                                a1 = S_.sb("a1", [128, 1040], F32)
                                a2 = S_.sb("a2", [128, 1040], F32)
                                invc = S_.sb("invc", [128, 4 * T], F32)
                                yTg = S_.sb("yTg", [128, 2, T], BF16)
                                wp = S_.sb("wp", [128, 4, 2, 256], BF16)
                                psc = S_.sb("psc", [128, 8], F32)
                                pS = S_.ps("pS", [128, 2, 2, 256])
                                pO = S_.ps("pO", [128, 4, 512])
                                pT = S_.ps("pT", [128, 2, 1024], BF16)
                                P.dma("sp", lamt[:], lam_in.rearrange("a b -> (a b)").partition_broadcast(128), "m0", writes=["lamt"])
                                P.dma("sp", sg8[:], subln_g.partition_broadcast(128), "m1", writes=["sg8"])
                                P.dma("sp", invc[:], invc_in.partition_broadcast(128), "m2", writes=["invc"])
                                P.dma("sp", psc[:], pool_scale, "m3", writes=["psc"])
                                P.dma("pool", wp[:], w_pool.rearrange("g (c p) d -> p g c d", p=128), "m4", writes=["wp"])
                                P.op("dve", lambda e: e.tensor_tensor(out=lamp[:], in0=lamt[:, 0:256], in1=lamt[:, 256:512], op=ALU.mult),
                                     reads=["lamt"], writes=["lamp"])
                                P.op("dve", lambda e: e.tensor_reduce(out=st2[:, 0:2], in_=lamp[:].rearrange("p (a b) -> p a b", a=2), axis=AX.X, op=ALU.add),
                                     reads=["lamp"], writes=["lsum"])
                                P.op("act", lambda e: e.activation(out=st2[:, 2:4], in_=st2[:, 0:2], func=AF.Exp), reads=["lsum"], writes=["lexp"])
                                P.op("dve", lambda e: e.tensor_tensor(out=st2[:, 4:5], in0=st2[:, 3:4], in1=st2[:, 2:3], op=ALU.subtract),
                                     reads=["lexp"], writes=["neglam"])
                                P.op("dve", lambda e: e.tensor_scalar_add(out=st2[:, 4:5], in0=st2[:, 4:5], scalar1=-0.2),
                                     reads=["neglam"], writes=["neglam"])
                                P.op("dve", lambda e: e.tensor_scalar_mul(out=sg8[:], in0=sg8[:], scalar1=0.8), reads=["sg8"], writes=["sg8"])
                                P.op("dve", lambda e: e.memset(vth[:, :, :, 256:258], 1.0), writes=[("vth1",)])
                                for cc_ in range(8):
                                    g = cc_ // 2
                                    c2 = cc_ % 2
                                    ubb = cc_ % 2
                                    row = 6144 + cc_ * 128
                                    P.dma("sp", ub[:, ubb, 16:1040], kv_src[row:row + 128, :], ("ub", ubb), reads=["kv_src"],
                                          writes=[("ub", ubb)])
                                    P.dma("sp", ub[:, ubb, 0:16], kv_dst[(24 + cc_ // 2) * 512 + (cc_ % 2) * 128:(24 + cc_ // 2) * 512 + (cc_ % 2) * 128 + 128, 1008:1024], ("ub", ubb), reads=["kv_dst"],
                                          writes=[("ub", ubb)])
                                    P.op("dve", lambda e, ubb=ubb: e.tensor_scalar_mul(out=ub[:, ubb, 0:16], in0=ub[:, ubb, 0:16],
                                                                                       scalar1=pc[:, 1:2]),
                                         reads=["pc"], writes=[("ub", ubb)])
                                    P.op("dve", lambda e, ubb=ubb: e.tensor_tensor(out=a1[:, 1:1040], in0=ub[:, ubb, 1:1040],
                                                                                   in1=ub[:, ubb, 0:1039], op=ALU.add),
                                         reads=[("ub", ubb)], writes=["a1"])
                                    cur = a1
                                    if g >= 1:
                                        P.op("dve", lambda e: e.tensor_tensor(out=a2[:, 3:1040], in0=a1[:, 3:1040], in1=a1[:, 1:1038], op=ALU.add),
                                             reads=["a1"], writes=["a2"])
                                        cur = a2
                                    if g >= 2:
                                        P.op("dve", lambda e: e.tensor_tensor(out=a1[:, 7:1040], in0=a2[:, 7:1040], in1=a2[:, 3:1036], op=ALU.add),
                                             reads=["a2"], writes=["a1"])
                                        cur = a1
                                    if g >= 3:
                                        P.op("dve", lambda e: e.tensor_tensor(out=a2[:, 15:1040], in0=a1[:, 15:1040], in1=a1[:, 7:1032], op=ALU.add),
                                             reads=["a1"], writes=["a2"])
                                        cur = a2
                                    ck = "a1" if cur is a1 else "a2"
                                    P.op("dve", lambda e, cur=cur, g=g: e.tensor_tensor(out=cur[:, 16:1040], in0=cur[:, 16:1040],
                                                                                        in1=invc[:, g * T:(g + 1) * T], op=ALU.mult),
                                         reads=["invc", ck], writes=[ck])
                                    P.op("dve", lambda e, cur=cur, ubb=ubb, c2=c2: e.tensor_tensor(out=yTg[:, c2, :], in0=cur[:, 16:1040],
                                                                                                    in1=ub[:, ubb, 16:1040], op=ALU.subtract),
                                         reads=[ck, ("ub", ubb)], writes=[("yTg", c2)])
                                    if c2 == 1:
                                        for dc in range(2):
                                            for half in range(2):
                                                for c3 in range(2):
                                                    P.op("pe", lambda e, g=g, c3=c3, dc=dc, half=half: e.matmul(
                                                        pO[:, 3, :], lhsT=wp[:, g, c3, dc * 128:(dc + 1) * 128],
                                                        rhs=yTg[:, c3, half * 512:(half + 1) * 512], start=(c3 == 0), stop=(c3 == 1)),
                                                         reads=["wp", ("yTg", 0), ("yTg", 1)], writes=[("pO", 3)])
                                                P.op("act", lambda e, g=g, dc=dc, half=half: e.activation(
                                                    out=mixT[:, g * 2 + dc, half * 512:(half + 1) * 512], in_=pO[:, 3, :], func=AF.Copy,
                                                    scale=psc[:, g * 2 + dc:g * 2 + dc + 1]),
                                                     reads=["psc"], writes=[("pO", 3), ("mixT", g * 2 + dc)])
                                ei = 0
                                fi = 0
                                for hd in range(12):
                                    sl = hd % 2
                                    rows = slice(hd * 256, (hd + 1) * 256)
                                    P.dma("sp", kth[:, sl, :, 0, :], kv_dst[hd * 512:hd * 512 + 256, :].rearrange("(m p) t -> p m t", p=128), ("kth", sl),
                                          reads=["kv_dst"], writes=[("kth", sl)])
                                    P.dma("sp", kth[:, sl, :, 1, :], kv_src[rows, :].rearrange("(m p) t -> p m t", p=128), ("kth", sl),
                                          reads=["kv_src"], writes=[("kth", sl)])
                                    P.dma("sp", vth[:, sl, 0:8, 0:256], vrem(hd).rearrange("(j p) c -> p j c", p=128), ("vth", sl),
                                          reads=["kv_dst"], writes=[("vth", sl)])
                                    P.dma("sp", vth[:, sl, 8:16, 0:256], vown(hd).rearrange("(j p) c -> p j c", p=128), ("vth", sl),
                                          reads=["kv_src"], writes=[("vth", sl)])
                                    for qp in range(4):
                                        blocks = [(0, kb) for kb in range(8)] + [(1, kb) for kb in range(2 * qp + 2)]
                                        for bi, (src, kb) in enumerate(blocks):
                                            sb_ = bi % 2
                                            eb = ei % 3
                                            ei += 1
                                            qlo = 128 if (src == 1 and kb == 2 * qp + 1) else 0
                                            for m in range(2):
                                                P.op("pe", lambda e, sl=sl, m=m, src=src, kb=kb, qp=qp, qlo=qlo, sb_=sb_, hd=hd: e.matmul(
                                                    pS[:, sb_, m, qlo:256], lhsT=kth[:, sl, m, src, kb * 128:(kb + 1) * 128],
                                                    rhs=qT[:, 2 * hd + m, qp * 256 + qlo:qp * 256 + 256], start=True, stop=True),
                                                     reads=[("kth", sl), ("qT", 2 * hd + m)], writes=[("pS", sb_)])
                                            bias_ap = pc[:, 0:1] if src == 0 else zero1[:, 0:1]
                                            P.op("act", lambda e, sb_=sb_, eb=eb, qlo=qlo, bias_ap=bias_ap: e.activation(
                                                out=E[:, eb, :, qlo:256], in_=pS[:, sb_, :, qlo:256], func=AF.Exp, bias=bias_ap),
                                                 reads=["pc", "zero1"], writes=[("pS", sb_), ("E", eb)])
                                            if src == 1 and kb >= 2 * qp:
                                                qs_d = kb - 2 * qp
                                                for m in range(2):
                                                    P.op("dve", lambda e, eb=eb, m=m, qs_d=qs_d: e.tensor_tensor(
                                                        out=E[:, eb, m, qs_d * 128:(qs_d + 1) * 128], in0=E[:, eb, m, qs_d * 128:(qs_d + 1) * 128],
                                                        in1=tril[:], op=ALU.mult), reads=["tril"], writes=[("E", eb)])
                                            for qs in range(2):
                                                if src == 1 and kb > 2 * qp + qs:
                                                    continue
                                                lastb = (src == 1 and kb == 2 * qp + qs)
                                                for m in range(2):
                                                    P.op("pe", lambda e, eb=eb, m=m, qs=qs, sl=sl, src=src, kb=kb, bi=bi, lastb=lastb: e.matmul(
                                                        pO[:, m * 2 + qs, 0:257], lhsT=E[:, eb, m, qs * 128:(qs + 1) * 128],
                                                        rhs=vth[:, sl, src * 8 + kb, 0:257], start=(bi == 0), stop=lastb),
                                                         reads=[("E", eb), ("vth", sl), ("vth1",)], writes=[("pO", m * 2 + qs)])
                                        for qs in range(2):
                                            qt = 2 * qp + qs
                                            fb = fi % 2
                                            fi += 1
                                            c0 = 8 + fb * 8
                                            P.op("dve", lambda e, qs=qs, c0=c0: e.reciprocal(out=st2[:, c0:c0 + 1], in_=pO[:, 0 + qs, 256:257]),
                                                 writes=[("pO", qs), ("fin", fb, 0)])
                                            P.op("dve", lambda e, qs=qs, c0=c0: e.reciprocal(out=st2[:, c0 + 1:c0 + 2], in_=pO[:, 2 + qs, 256:257]),
                                                 writes=[("pO", 2 + qs), ("fin", fb, 1)])
                                            P.op("dve", lambda e, c0=c0: e.tensor_tensor(out=st2[:, c0 + 2:c0 + 3], in0=st2[:, c0 + 1:c0 + 2],
                                                                                         in1=st2[:, 4:5], op=ALU.mult),
                                                 reads=[("fin", fb, 1), "neglam"], writes=[("fin", fb, 2)])
                                            P.op("act", lambda e, qs=qs, fb=fb, c0=c0: e.activation(out=o1[:, fb, :], in_=pO[:, 0 + qs, 0:256],
                                                                                                    func=AF.Copy, scale=st2[:, c0:c0 + 1]),
                                                 reads=[("fin", fb, 0)], writes=[("pO", qs), ("o1", fb)])
                                            P.op("dve", lambda e, qs=qs, fb=fb, c0=c0: e.scalar_tensor_tensor(
                                                out=o1[:, fb, :], in0=pO[:, 2 + qs, 0:256], scalar=st2[:, c0 + 2:c0 + 3], in1=o1[:, fb, :],
                                                op0=ALU.mult, op1=ALU.add),
                                                 reads=[("fin", fb, 2)], writes=[("pO", 2 + qs), ("o1", fb)])
                                            P.op("dve", lambda e, c0=c0: e.memset(st2[:, c0 + 3:c0 + 4], 0.0), writes=[("fin", fb, 3)])
                                            P.op("act", lambda e, fb=fb, c0=c0: e.activation(out=junk3[:], in_=o1[:, fb, :], func=AF.Square,
                                                                                             accum_out=st2[:, c0 + 3:c0 + 4]),
                                                 reads=[("o1", fb)], writes=["junk3", ("fin", fb, 3)])
                                            rstd_ops(st2[:, c0 + 3:c0 + 4], st2[:, c0 + 4:c0 + 5], 256, [("fin", fb, 3)], [("fin", fb, 4)])
                                            P.op("dve", lambda e, fb=fb, c0=c0: e.scalar_tensor_tensor(
                                                out=yb[:, fb, :], in0=o1[:, fb, :], scalar=st2[:, c0 + 4:c0 + 5], in1=sg8[:],
                                                op0=ALU.mult, op1=ALU.mult),
                                                 reads=[("fin", fb, 4), ("o1", fb), "sg8"], writes=[("yb", fb)])
                                            for j in range(2):
                                                P.op("pe", lambda e, fb=fb, j=j: e.transpose(pT[:, fb, j * 128:(j + 1) * 128], yb[:, fb, j * 128:(j + 1) * 128], ident[:]),
                                                     reads=[("yb", fb), "ident"], writes=[("pT", fb)])
                                            dst = mixT[:, 8 + 2 * hd:8 + 2 * hd + 2, qt * 128:(qt + 1) * 128]
                                            P.op("act", lambda e, fb=fb, dst=dst: e.copy(out=dst, in_=pT[:, fb, 0:256].rearrange("p (j t) -> p j t", t=128)),
                                                 writes=[("pT", fb), ("mixT", 8 + 2 * hd), ("mixT", 9 + 2 * hd)])
                                P.barrier()
                    for th in range(2):
                        with scope() as S_:
                            Fm = S_.sb("Fm", [128, 4, D], F32)
                            with scope() as S_:
                                wbo = S_.sb("wbo", [128, 2, 8, 512], BF16)
                                po = S_.ps("po", [128, 8, 512])
                                it = 0
                                for cp in range(4):
                                    for g0 in range(0, KC, 8):
                                        for c2 in range(2):
                                            ct = cp * 2 + c2
                                            sl = it % 2
                                            it += 1
                                            P.dma("pool", wbo[:, sl], w_out[g0 * 128:(g0 + 8) * 128, ct * 512:(ct + 1) * 512]
                                                  .rearrange("(j p) f -> p j f", p=128), ("wbo", sl), writes=[("wbo", sl)])
                                            for j in range(8):
                                                fcn = g0 + j
                                                for t4 in range(4):
                                                    tt = th * 4 + t4
                                                    P.op("pe", lambda e, sl=sl, j=j, fcn=fcn, tt=tt, t4=t4, c2=c2: e.matmul(
                                                        po[:, c2 * 4 + t4, :], lhsT=mixT[:, fcn, tt * 128:(tt + 1) * 128],
                                                        rhs=wbo[:, sl, j, :], start=(fcn == 0), stop=(fcn == KC - 1)),
                                                         reads=[("wbo", sl), ("mixT", fcn)], writes=[("po", c2 * 4 + t4)])
                                    for c2 in range(2):
                                        ct = cp * 2 + c2
                                        for t4 in range(4):
                                            dst = Fm[:, t4, ct * 512:(ct + 1) * 512]
                                            pb = c2 * 4 + t4
                                            if t4 % 2 == 0:
                                                P.op("act", lambda e, pb=pb, dst=dst: e.copy(out=dst, in_=po[:, pb, :]),
                                                     writes=[("po", pb), ("F", t4)])
                                            else:
                                                P.op("dve", lambda e, pb=pb, dst=dst: e.tensor_copy(out=dst, in_=po[:, pb, :]),
                                                     writes=[("po", pb), ("F", t4)])
                                P.barrier()
                            epilogue(1, Fm, [th * 4 + t4 for t4 in range(4)], x1, mixer_dst, 1.0)

            if stage >= 3:
                ffn(2, 1, x2, out)


        try:
            dbg_stop("ada", mod_dst[:, 0:4096], 32, 4096)
            main_body()
        except _Stop:
            pass
        P.emit()
    return nc


_NC_CACHE = {}


_STOP = None


def kernel(x, c, w_ada, b_ada, pre_norm_g, post_norm_g, ffn1_w1, ffn1_w3, ffn1_w2, w_in, w_pool, pool_scale,
           lambda_q1, lambda_k1, lambda_q2, lambda_k2, subln_g, w_out, ffn2_w1, ffn2_w3, ffn2_w2):
    f32 = np.float32
    A = lambda a: np.ascontiguousarray(np.asarray(a, dtype=f32))
    x = A(x)
    c = A(c)
    w_ada = np.asarray(w_ada, dtype=f32)[0]
    b_ada = A(b_ada)[0]
    cT = np.ascontiguousarray(c.T.reshape(KC, 128, 4).transpose(1, 0, 2))
    ident = np.eye(128, dtype=f32).astype(ml_dtypes.bfloat16)
    tril = np.triu(np.ones((128, 128), dtype=f32)).astype(ml_dtypes.bfloat16)
    lam = np.ascontiguousarray(np.stack([A(lambda_q1)[0], A(lambda_q2)[0], A(lambda_k1)[0], A(lambda_k2)[0]], 0))
    shared = {
        "cT": cT, "pre_g": A(pre_norm_g)[0], "post_g": A(post_norm_g)[0],
        "w1a": A(ffn1_w1)[0], "w3a": A(ffn1_w3)[0], "w2a": A(ffn1_w2)[0],
        "w1b": A(ffn2_w1)[0], "w3b": A(ffn2_w3)[0], "w2b": A(ffn2_w2)[0],
        "w_in": A(w_in)[0], "w_pool": A(w_pool)[0], "pool_scale": np.ascontiguousarray(A(pool_scale)[0].reshape(8, 128).T), "lam": lam,
        "subln_g": A(subln_g)[0], "w_out": A(w_out)[0], "ident": ident, "tril": tril,
    }
    in_maps = []
    wins = np.array([2, 4, 8, 16], dtype=np.int64)
    for core in range(NCORES):
        b, hf = core // 2, core % 2
        sel = np.zeros((32, 8, 128), dtype=f32)
        for r in range(8):
            sel[r * 4 + b, r, :] = 1.0
        pcv = np.zeros((128, 4), dtype=f32)
        pcv[:, 0] = 0.0 if hf == 1 else -30000.0
        pcv[:, 1] = 1.0 if hf == 1 else 0.0
        tpos = np.arange(T, dtype=np.int64) + hf * T
        cnt = np.minimum(tpos[None, :] + 1, wins[:, None]).astype(f32)
        m = dict(shared)
        m["x"] = np.ascontiguousarray(x[b, hf * T:(hf + 1) * T, :])
        m["wada"] = np.ascontiguousarray(w_ada[:, core * ADA_SH:(core + 1) * ADA_SH])
        m["bada"] = np.ascontiguousarray(b_ada[core * ADA_SH:(core + 1) * ADA_SH])
        m["sel"] = sel
        m["percore"] = pcv
        m["invcnt"] = np.ascontiguousarray((1.0 / cnt).astype(f32).reshape(-1))
        in_maps.append(m)
    if (STAGE, _STOP) not in _NC_CACHE:
        _NC_CACHE[(STAGE, _STOP)] = build_nc(STAGE, _STOP)
    nc = _NC_CACHE[(STAGE, _STOP)]
    res = run_bass_kernel_spmd(nc, in_maps, core_ids=list(range(NCORES)))
    outp = np.empty((4, 2048, D), dtype=f32)
    for core in range(NCORES):
        b, hf = core // 2, core % 2
        outp[b, hf * T:(hf + 1) * T, :] = np.asarray(res.results[core]["out"])
    return outp
```
